# Optimizing a Trainium2 kernel written in Bass

```python
import jax, jax.numpy as jnp
from jax import lax
import numpy as np

D_MODEL = 2048
BATCH = 2
SEQ = 8192
DEPTH = 2

GRID_W = 64
CTX_LEN = 256
N_MIXERS = 2
N_HGRN_LAYERS = (DEPTH + N_MIXERS - 1) // N_MIXERS
N_ATTN_LAYERS = DEPTH // N_MIXERS
HGRN_HEADS = 16
HGRN_HEAD_K = D_MODEL // HGRN_HEADS
HGRN_HEAD_V = D_MODEL // HGRN_HEADS
HGRN_CHUNK = 64
ATTN_HEADS = 16
ATTN_KV_HEADS = 4
ATTN_GROUP = ATTN_HEADS // ATTN_KV_HEADS
HEAD_DIM = D_MODEL // ATTN_HEADS
AXIS_DIM = HEAD_DIM // 2
ROPE_THETA = 10000.0
Q_BLOCK = 128
D_FF = ((8 * D_MODEL // 3 + 255) // 256) * 256
CONV_W = 3
N_MOD = 6
EPS = 1e-6

kernel_name = 'hybrid_hgrn2_gqa_convffn_dit'


def rms_norm(x, gain):
    x32 = x.astype(jnp.float32)
    y = x32 * lax.rsqrt(jnp.mean(x32 * x32, axis=-1, keepdims=True) + EPS)
    return (y * gain.astype(jnp.float32)).astype(x.dtype)


def modulate(x, shift, scale):
    return x * (1 + scale) + shift


def dwconv_centred(h, w, b):
    T = h.shape[1]
    hp = jnp.pad(h, ((0, 0), (CONV_W // 2, CONV_W // 2), (0, 0)))
    return sum(hp[:, d:d + T] * w[d] for d in range(CONV_W)) + b


def conv_ffn(h, w_in, conv_w, conv_b, w_out):
    gate, up = jnp.split(h @ w_in, 2, axis=-1)
    return (jax.nn.silu(dwconv_centred(gate, conv_w, conv_b)) * up) @ w_out


def gla_chunk_scan(q, k, v, log_f, s0):
    B, H, T, _ = q.shape
    n_chunks = T // HGRN_CHUNK

    def chunks(a):
        return jnp.moveaxis(a.reshape(B, H, n_chunks, HGRN_CHUNK, a.shape[-1]), 2, 0)

    within = jnp.tril(jnp.ones((HGRN_CHUNK, HGRN_CHUNK), dtype=bool))[:, :, None]

    def step(S, inp):
        qc, kc, vc, lfc = inp
        b = jnp.cumsum(lfc, axis=2)
        rel = b[:, :, :, None, :] - b[:, :, None, :, :]
        decay = jnp.exp(jnp.where(within, rel, -jnp.inf))
        scores = jnp.einsum('bhtk,bhsk,bhtsk->bhts', qc, kc, decay)
        o = (jnp.einsum('bhts,bhsv->bhtv', scores, vc)
             + jnp.einsum('bhtk,bhkv->bhtv', qc * jnp.exp(b), S))
        b_end = b[:, :, -1:, :]
        S = (jnp.exp(b_end[:, :, 0, :, None]) * S
             + jnp.einsum('bhsk,bhsv->bhkv', kc * jnp.exp(b_end - b), vc))
        return S, o

    S, o = lax.scan(step, s0, (chunks(q), chunks(k), chunks(v), chunks(log_f)))
    return jnp.moveaxis(o, 0, 2).reshape(B, H, T, v.shape[-1]), S


def hgrn2_mixer(h_lat, h_ctx, w_in, lower_bound, o_gain, w_out, need_ctx):
    def project(h):
        B, T, _ = h.shape
        q, f_fw, f_bw, v, g = jnp.split(h @ w_in, 5, axis=-1)

        def heads(a):
            return a.reshape(B, T, HGRN_HEADS, -1).transpose(0, 2, 1, 3).astype(jnp.float32)

        dirs = []
        for f_raw, lb in ((f_fw, lower_bound[0]), (f_bw, lower_bound[1])):
            f = lb + (1 - lb) * jax.nn.sigmoid(f_raw.astype(jnp.float32))
            dirs.append((heads(1 - f), heads(jnp.log(f))))
        return heads(jax.nn.silu(q)) * HGRN_HEAD_K ** -0.5, dirs, heads(v), g

    def flip(a):
        return a[:, :, ::-1]

    def readout(o, g):
        B, H, T, V = o.shape
        o = rms_norm(o, o_gain).transpose(0, 2, 1, 3).reshape(B, T, H * V).astype(g.dtype)
        return (o * jax.nn.silu(g)) @ w_out

    q_c, ((k_cf, lf_cf), (k_cb, lf_cb)), v_c, g_c = project(h_ctx)
    q_l, ((k_lf, lf_lf), (k_lb, lf_lb)), v_l, g_l = project(h_lat)
    s0 = jnp.zeros((h_lat.shape[0], HGRN_HEADS, HGRN_HEAD_K, HGRN_HEAD_V), jnp.float32)
    o_cf, s_fw = gla_chunk_scan(q_c, k_cf, v_c, lf_cf, s0)
    o_cb, s_bw = gla_chunk_scan(flip(q_c), flip(k_cb), flip(v_c), flip(lf_cb), s0)
    o_lf, _ = gla_chunk_scan(q_l, k_lf, v_l, lf_lf, s_fw)
    o_lb, _ = gla_chunk_scan(flip(q_l), flip(k_lb), flip(v_l), flip(lf_lb), s_bw)
    y_lat = readout(o_lf + flip(o_lb), g_l)
    y_ctx = readout(o_cf + flip(o_cb), g_c) if need_ctx else None
    return y_lat, y_ctx


def axial_rope_tables(rows, cols):
    inv_freq = ROPE_THETA ** (-jnp.arange(0, AXIS_DIM, 2, dtype=jnp.float32) / AXIS_DIM)
    ang = jnp.concatenate([rows[:, None] * inv_freq, cols[:, None] * inv_freq], axis=-1)
    return jnp.cos(ang), jnp.sin(ang)


def apply_axial_rope(x, cos, sin):
    x32 = x.astype(jnp.float32)
    x1, x2 = x32[..., 0::2], x32[..., 1::2]
    c, s = cos[:, None, :], sin[:, None, :]
    return jnp.stack([x1 * c - x2 * s, x1 * s + x2 * c], axis=-1).reshape(x.shape).astype(x.dtype)


def gqa_mixer(h_lat, h_ctx, w_qkv, q_gain, k_gain, w_out, cos, sin, need_ctx):
    def project(h):
        B, T, _ = h.shape
        p = (h @ w_qkv).reshape(B, T, ATTN_HEADS + 2 * ATTN_KV_HEADS, HEAD_DIM)
        k = rms_norm(p[:, :, ATTN_HEADS:ATTN_HEADS + ATTN_KV_HEADS], k_gain)
        return p[:, :, :ATTN_HEADS], k, p[:, :, ATTN_HEADS + ATTN_KV_HEADS:]

    def attend(q, k, v):
        s = jnp.einsum('bqkgd,bskd->bkgqs', q, k, preferred_element_type=jnp.float32) * HEAD_DIM ** -0.5
        p = jax.nn.softmax(s, axis=-1).astype(v.dtype)
        return jnp.einsum('bkgqs,bskd->bqkgd', p, v)

    B, T, D = h_lat.shape
    q_l, k_l, v_l = project(h_lat)
    q_c, k_c, v_c = project(h_ctx)
    q_l = apply_axial_rope(rms_norm(q_l, q_gain), cos, sin)
    q_blocks = jnp.moveaxis(
        q_l.reshape(B, T // Q_BLOCK, Q_BLOCK, ATTN_KV_HEADS, ATTN_GROUP, HEAD_DIM), 1, 0)
    k_all = jnp.concatenate([k_c, apply_axial_rope(k_l, cos, sin)], axis=1)
    v_all = jnp.concatenate([v_c, v_l], axis=1)
    o_l = lax.map(lambda qb: attend(qb, k_all, v_all), q_blocks)
    y_lat = jnp.moveaxis(o_l, 0, 1).reshape(B, T, D) @ w_out
    if need_ctx:
        Tc = h_ctx.shape[1]
        q_c = rms_norm(q_c, q_gain).reshape(B, Tc, ATTN_KV_HEADS, ATTN_GROUP, HEAD_DIM)
        y_ctx = attend(q_c, k_c, v_c).reshape(B, Tc, D) @ w_out
    else:
        y_ctx = None
    return y_lat, y_ctx


def setup_inputs(seed: int = 0) -> dict:
    key = jax.random.key(seed)
    ks = iter(jax.random.split(key, 24))

    def normal(shape, scale):
        return jax.random.normal(next(ks), shape, jnp.float32) * scale

    def gain(shape):
        return 1.0 + normal(shape, 0.05)

    D = D_MODEL
    return {
        'x': normal((BATCH, SEQ, D), 1.0),
        'c': normal((BATCH, D), 1.0),
        'ctx': normal((BATCH, CTX_LEN, D), 1.0),
        'c_ctx': normal((D,), 1.0),
        'ada_w': normal((DEPTH, D, N_MOD * D), D ** -0.5),
        'ada_b': normal((DEPTH, N_MOD * D), 0.02),
        'norm_mix_pre': gain((DEPTH, D)),
        'norm_mix_post': gain((DEPTH, D)),
        'norm_ffn_pre': gain((DEPTH, D)),
        'norm_ffn_post': gain((DEPTH, D)),
        'hgrn_w_in': normal((N_HGRN_LAYERS, D, 5 * D), D ** -0.5),
        'hgrn_lb_logits': normal((2, N_HGRN_LAYERS + 1, HGRN_HEADS * HGRN_HEAD_K), 0.1),
        'hgrn_o_norm': gain((N_HGRN_LAYERS, HGRN_HEAD_V)),
        'hgrn_w_out': normal((N_HGRN_LAYERS, HGRN_HEADS * HGRN_HEAD_V, D), D ** -0.5),
        'attn_w_qkv': normal((N_ATTN_LAYERS, D, (ATTN_HEADS + 2 * ATTN_KV_HEADS) * HEAD_DIM), D ** -0.5),
        'attn_q_norm': gain((N_ATTN_LAYERS, HEAD_DIM)),
        'attn_k_norm': gain((N_ATTN_LAYERS, HEAD_DIM)),
        'attn_w_out': normal((N_ATTN_LAYERS, ATTN_HEADS * HEAD_DIM, D), D ** -0.5),
        'ffn_w_in': normal((DEPTH, D, 2 * D_FF), D ** -0.5),
        'ffn_conv_w': normal((DEPTH, CONV_W, D_FF), CONV_W ** -0.5),
        'ffn_conv_b': normal((DEPTH, D_FF), 0.02),
        'ffn_w_out': normal((DEPTH, D_FF, D), D_FF ** -0.5),
    }


def reference(x, c, ctx, c_ctx, ada_w, ada_b, norm_mix_pre, norm_mix_post, norm_ffn_pre, norm_ffn_post,
              hgrn_w_in, hgrn_lb_logits, hgrn_o_norm, hgrn_w_out,
              attn_w_qkv, attn_q_norm, attn_k_norm, attn_w_out,
              ffn_w_in, ffn_conv_w, ffn_conv_b, ffn_w_out):
    T = x.shape[1]
    ROWS = T // GRID_W
    rows = jnp.repeat(jnp.arange(ROWS, dtype=jnp.float32), GRID_W)
    cols = jnp.tile(jnp.arange(GRID_W, dtype=jnp.float32), ROWS)
    cos, sin = axial_rope_tables(rows, cols)
    lower_bounds = jnp.cumsum(jax.nn.softmax(hgrn_lb_logits.astype(jnp.float32), axis=1), axis=1)
    silu_c = jax.nn.silu(c)
    silu_cc = jax.nn.silu(c_ctx)
    x_lat, x_ctx = x, ctx
    for i in range(DEPTH):
        j = i // N_MIXERS
        last = i == DEPTH - 1
        sh_ml, sc_ml, gt_ml, sh_fl, sc_fl, gt_fl = jnp.split(
            (silu_c @ ada_w[i] + ada_b[i])[:, None, :], N_MOD, axis=-1)
        sh_mc, sc_mc, gt_mc, sh_fc, sc_fc, gt_fc = jnp.split(
            silu_cc @ ada_w[i] + ada_b[i], N_MOD, axis=-1)
        h_l = modulate(rms_norm(x_lat, norm_mix_pre[i]), sh_ml, sc_ml)
        h_c = modulate(rms_norm(x_ctx, norm_mix_pre[i]), sh_mc, sc_mc)
        if i % N_MIXERS == 0:
            y_l, y_c = hgrn2_mixer(h_l, h_c, hgrn_w_in[j], lower_bounds[:, j], hgrn_o_norm[j],
                                   hgrn_w_out[j], not last)
        else:
            y_l, y_c = gqa_mixer(h_l, h_c, attn_w_qkv[j], attn_q_norm[j], attn_k_norm[j],
                                 attn_w_out[j], cos, sin, not last)
        x_lat = x_lat + gt_ml * rms_norm(y_l, norm_mix_post[i])
        h_l = modulate(rms_norm(x_lat, norm_ffn_pre[i]), sh_fl, sc_fl)
        f_l = conv_ffn(h_l, ffn_w_in[i], ffn_conv_w[i], ffn_conv_b[i], ffn_w_out[i])
        x_lat = x_lat + gt_fl * rms_norm(f_l, norm_ffn_post[i])
        if not last:
            x_ctx = x_ctx + gt_mc * rms_norm(y_c, norm_mix_post[i])
            h_c = modulate(rms_norm(x_ctx, norm_ffn_pre[i]), sh_fc, sc_fc)
            f_c = conv_ffn(h_c, ffn_w_in[i], ffn_conv_w[i], ffn_conv_b[i], ffn_w_out[i])
            x_ctx = x_ctx + gt_fc * rms_norm(f_c, norm_ffn_post[i])
    return x_lat
```

```python
import contextlib
import numpy as np
import concourse.bass as bass
import concourse.mybir as mybir
from concourse.bass_utils import run_bass_kernel_spmd

F32 = mybir.dt.float32
BF16 = mybir.dt.bfloat16
AF = mybir.ActivationFunctionType
ALU = mybir.AluOpType
AX = mybir.AxisListType

SAME_ENG_SYNC = True

TL, TC, TT, D, KC = 2048, 256, 2304, 2048, 16
DFF, FT = 5632, 44
EPS = 1e-6
NCORE = 8
G4 = [[0, 1, 2, 3], [4, 5, 6, 7]]
G8 = [list(range(8))]


class Buf:
    __slots__ = ("w", "r")

    def __init__(self):
        self.w = None
        self.r = {}


class T:
    __slots__ = ("t", "b")

    def __init__(self, t):
        self.t = t
        self.b = Buf()


class KB:
    def __init__(self, nc, ndma=8):
        self.nc = nc
        self.eng = {"pe": nc.tensor, "act": nc.scalar, "dve": nc.vector, "pool": nc.gpsimd, "sp": nc.sync}
        self.semh = {}
        self.cnt = {}
        for k in ["pe", "act", "dve", "pool"]:
            self.semh[k] = nc.alloc_semaphore("s_" + k)
            self.cnt[k] = 0
        self.seen = {k: {} for k in self.eng}
        self.dq, self.didx, self.dval = {}, {}, {}
        for q in ["sp", "pool"]:
            self.dq[q] = [f"d_{q}_{i}" for i in range(ndma if q == "sp" else 3)]
            self.didx[q] = 0
            for key in self.dq[q]:
                self.semh[key] = nc.alloc_semaphore(key)
                self.dval[key] = 0
        self.ncc = 0
        self.ninst = 0

    def _wait(self, e, deps):
        for (k, v) in sorted(deps):
            if k == e and (e == "pe" or not SAME_ENG_SYNC):
                continue
            if self.seen[e].get(k, 0) >= v:
                continue
            self.eng[e].wait_ge(self.semh[k], v)
            self.seen[e][k] = v

    @staticmethod
    def _deps(reads, writes):
        deps = set()
        for b in reads:
            if b.w:
                deps.add(b.w)
        for b in writes:
            if b.w:
                deps.add(b.w)
            deps.update(b.r.items())
        return deps

    def op(self, e, fn, reads=(), writes=()):
        reads = [x.b for x in reads]
        writes = [x.b for x in writes]
        self._wait(e, self._deps(reads, writes))
        ins = fn()
        self.cnt[e] += 1
        ins.then_inc(self.semh[e], 1)
        c = self.cnt[e]
        for b in reads:
            b.r[e] = c
        for b in writes:
            b.w = (e, c)
            b.r = {}
        self.ninst += 1

    def dma(self, q, out, in_, reads=(), writes=(), **kw):
        reads = [x.b for x in reads]
        writes = [x.b for x in writes]
        i = self.didx[q]
        self.didx[q] = (i + 1) % len(self.dq[q])
        key = self.dq[q][i]
        prev = self.dval[key]
        deps = self._deps(reads, writes)
        if prev > 0:
            deps.add((key, prev))
        self._wait(q, deps)
        self.eng[q].dma_start(out=out, in_=in_, **kw).then_inc(self.semh[key], 16)
        val = prev + 16
        self.dval[key] = val
        for b in reads:
            b.r[key] = val
        for b in writes:
            b.w = (key, val)
            b.r = {}
        self.ninst += 1

    def collective(self, ins, outs, groups, reads=(), writes=()):
        reads = [x.b for x in reads]
        writes = [x.b for x in writes]
        key = f"cc_{self.ncc}"
        self.ncc += 1
        self.semh[key] = self.nc.alloc_semaphore(key)
        self._wait("pool", self._deps(reads, writes))
        self.nc.gpsimd.collective_compute("AllGather", ALU.bypass, ins=ins, outs=outs,
                                          replica_groups=groups).then_inc(self.semh[key])
        for b in reads:
            b.r[key] = 1
        for b in writes:
            b.w = (key, 1)
            b.r = {}

    def barrier(self):
        toks = set()
        for k in ["pe", "act", "dve", "pool"]:
            if self.cnt[k] > 0:
                toks.add((k, self.cnt[k]))
        for key, v in self.dval.items():
            if v > 0:
                toks.add((key, v))
        for i in range(self.ncc):
            toks.add((f"cc_{i}", 1))
        for e in ["pe", "act", "dve", "pool", "sp"]:
            for (k, v) in sorted(toks):
                if self.seen[e].get(k, 0) >= v:
                    continue
                self.eng[e].wait_ge(self.semh[k], v)
                self.seen[e][k] = v


def blocks(n, step=512):
    return [(i, min(step, n - i)) for i in range(0, n, step)]


def eblocks(n):
    nb = -(-n // 512)
    sz = -(-n // nb)
    sz += sz % 2
    return [(i, min(sz, n - i)) for i in range(0, n, sz)]


def build(stop=None, dbg=()):
    nc = bass.Bass("TRN2", target_bir_lowering=False)
    kb = KB(nc)
    V, S_, P_ = nc.vector, nc.scalar, nc.tensor

    def ein(name, shape, dt=F32):
        return T(nc.dram_tensor(name, list(shape), dt, kind="ExternalInput").ap())

    def dram(name, shape, dt=F32):
        kind = "ExternalOutput" if name in dbg else "Internal"
        return T(nc.dram_tensor(name, list(shape), dt, kind=kind).ap())

    x_in = ein("x_in", [TL, D])
    ctx_in = ein("ctx_in", [TC, D])
    cvec = ein("cvec", [128, KC, 2])
    WGRP = {"g12288": (12288, [("adaw0", 2048), ("adaw1", 2048)]),
            "ghg": (10240, [("hgwin", 2048)]),
            "g2048": (2048, [("hgwout", 2048), ("aout", 2048), ("qkv", 3072), ("fout0", 6144), ("fout1", 6144)]),
            "g4096": (4096, [("fin0", 6144), ("fin1", 6144)])}
    WS, WFG, WBG, WINFO, WF = {}, {}, {}, {}, {}
    for gname, (cols, ws) in WGRP.items():
        rpr = sum(r for _, r in ws) // 8
        WS[gname] = ein(gname + "_s", [rpr, cols])
        off = 0
        for nm, r in ws:
            WINFO[nm] = (gname, rpr, off, r // 8)
            off += r // 8
    adab = ein("adab", [2, 12288])
    nrm = {k: ein(k, [2, D]) for k in ["n_mpre", "n_mpost", "n_fpre", "n_fpost"]}
    lbl = ein("lbl", [2, 2, 2048])
    ogain = ein("ogain", [128, 1])
    qgain = ein("qgain", [128, 1])
    kgain = ein("kgain", [128, 1])
    convw = ein("convw", [2, 3, DFF])
    convb = ein("convb", [2, DFF])
    c_ident = ein("c_ident", [128, 128])
    c_maskf = ein("c_maskf", [128, 512])
    c_maskb = ein("c_maskb", [128, 512])
    c_rmask = ein("c_rmask", [128, TT])
    c_e4 = ein("c_e4", [128, 4])
    c_eb = ein("c_eb", [128, 2])
    c_hsel = ein("c_hsel", [16, 2])
    c_perm = ein("c_perm", [128, 128])
    c_ropec = ein("c_ropec", [128, TL])
    c_ropes = ein("c_ropes", [128, TL])
    y_out = T(nc.dram_tensor("y_out", [TL, D], F32, kind="ExternalOutput").ap())

    for gname, (cols, ws) in WGRP.items():
        rpr = sum(r for _, r in ws) // 8
        WFG[gname] = dram(gname + "_f", [8 * rpr, cols])
        WBG[gname] = dram(gname + "_b", [rpr, cols])
        for nm, r in ws:
            WF[nm] = WFG[gname]

    def wrows(nm, row0, n):
        gname, rpr, off, rw = WINFO[nm]
        r = row0 // rw
        assert (row0 + n - 1) // rw == r
        st0 = r * rpr + off + (row0 - r * rw)
        return WFG[gname].t[st0:st0 + n, :]

    mods = dram("mods", [2, 2, 6 * D])
    x1 = dram("x1", [TT, D])
    x2 = dram("x2", [TT, D])
    x3 = dram("x3", [TL, D])
    zT = dram("zT", [18, 128, KC, 128], BF16)
    hid = dram("hid", [18, 128, FT, 128], BF16)
    ystash = dram("ystash", [TT, 1536])
    hTd2 = dram("hTd2", [128, KC, TT + 4], BF16)
    hTd3 = dram("hTd3", [128, KC, TT], BF16)
    hTd4 = dram("hTd4", [128, KC, TL + 2], BF16)
    st_loc = dram("st_loc", [4096 + 32, 128])
    st_all = dram("st_all", [8 * (4096 + 32), 128])
    halo_loc = dram("halo_loc", [2, D])
    halo_all = dram("halo_all", [16, D])
    kv_loc = dram("kv_loc", [8192, 128])
    kv_all = dram("kv_all", [8 * 8192, 128])
    kv_loc_b = kv_loc.t.bitcast(BF16)
    kv_all_b = kv_all.t.bitcast(BF16)
    k_loc_v = kv_loc_b[0:4096, :].rearrange("(a b) c -> a (b c)", b=8)
    v_loc_v = kv_loc_b[4096:8192, :].rearrange("(a b) c -> a (b c)", b=2)

    def k_all_v(rr):
        return kv_all_b[rr * 8192:rr * 8192 + 4096, :].rearrange("(a b) c -> a (b c)", b=8)

    def v_all_v(rr):
        return kv_all_b[rr * 8192 + 4096:(rr + 1) * 8192, :].rearrange("(a b) c -> a (b c)", b=2)

    k_ctx = dram("k_ctx", [4 * 128, TC], BF16)
    v_ctx = dram("v_ctx", [TC, 512], BF16)

    for gname, (cols, ws) in WGRP.items():
        rpr = sum(r for _, r in ws) // 8
        for r0 in range(0, rpr, 128):
            r1 = min(rpr, r0 + 128)
            kb.dma("sp", WBG[gname].t[r0:r1, :], WS[gname].t[r0:r1, :], reads=[WS[gname]], writes=[WBG[gname]])
    for gname in WGRP:
        kb.collective([WBG[gname].t.opt()], [WFG[gname].t.opt()], G8, reads=[WBG[gname]], writes=[WFG[gname]])

    ps = [T(nc.alloc_psum_tensor(f"ps{i}", [128, 512], F32)) for i in range(7)]
    psb = T(nc.alloc_psum_tensor("psb", [128, 1024], BF16))

    es = contextlib.ExitStack()

    uid = [0]

    def sb(name, shape, dt=F32, stack=None):
        uid[0] += 1
        g = nc.sbuf_tensor(f"{name}_{uid[0]}", list(shape), dt)
        t = (stack or es).enter_context(g)
        return T(t)

    ident = sb("ident", [128, 128])
    identb = sb("identb", [128, 128], BF16)
    ones_f = sb("ones_f", [128, 128])
    ones_b = sb("ones_b", [128, 128], BF16)
    e4 = sb("e4", [128, 4])
    eb = sb("eb", [128, 2])
    gvec = sb("gvec", [128, 3])
    kb.dma("sp", ident.t[:], c_ident.t[:, :], writes=[ident])
    kb.dma("sp", e4.t[:], c_e4.t[:, :], writes=[e4])
    kb.dma("sp", eb.t[:], c_eb.t[:, :], writes=[eb])
    kb.dma("sp", gvec.t[:, 0:1], ogain.t[:, :], writes=[gvec])
    kb.dma("sp", gvec.t[:, 1:2], qgain.t[:, :], writes=[gvec])
    kb.dma("sp", gvec.t[:, 2:3], kgain.t[:, :], writes=[gvec])
    kb.op("dve", lambda: V.tensor_copy(out=identb.t[:], in_=ident.t[:]), reads=[ident], writes=[identb])
    kb.op("dve", lambda: V.memset(ones_f.t[:], 1.0), writes=[ones_f])
    kb.op("dve", lambda: V.memset(ones_b.t[:], 1.0), writes=[ones_b])

    with contextlib.ExitStack() as st:
        cv = sb("cv", [128, KC, 2], stack=st)
        sc = sb("sc", [128, KC, 2], stack=st)
        wt2 = [sb(f"adawt{i}", [128, KC, 512], stack=st) for i in range(2)]
        bt2 = [sb(f"adabt{i}", [2, 512], stack=st) for i in range(2)]
        mo2 = [sb(f"adamo{i}", [2, 512], stack=st) for i in range(2)]
        kb.dma("sp", cv.t[:], cvec.t[:, :, :], writes=[cv])
        kb.op("act", lambda: S_.activation(out=sc.t[:], in_=cv.t[:], func=AF.Silu), reads=[cv], writes=[sc])
        it = 0
        for l in range(2):
            wf = WF[f"adaw{l}"]
            for cb in range(24):
                wt, bt, mo, pp = wt2[it % 2], bt2[it % 2], mo2[it % 2], ps[it % 2]
                it += 1
                for kc in range(KC):
                    kb.dma("sp", wt.t[:, kc, :], wrows(f"adaw{l}", kc * 128, 128)[:, cb * 512:(cb + 1) * 512],
                           reads=[wf], writes=[wt])
                kb.dma("sp", bt.t[:], adab.t[l, cb * 512:(cb + 1) * 512].partition_broadcast(2), writes=[bt])
                for kc in range(KC):
                    kb.op("pe", lambda kc=kc: P_.matmul(pp.t[0:2, :], lhsT=sc.t[:, kc, :], rhs=wt.t[:, kc, :],
                                                        start=(kc == 0), stop=(kc == KC - 1)),
                          reads=[sc, wt], writes=[pp])
                kb.op("dve", lambda: V.tensor_tensor(out=mo.t[:], in0=pp.t[0:2, :], in1=bt.t[:], op=ALU.add),
                      reads=[pp, bt], writes=[mo])
                kb.dma("sp", mods.t[l, :, cb * 512:(cb + 1) * 512], mo.t[:], reads=[mo], writes=[mods])
        kb.barrier()
    if stop == "mods":
        kb.barrier()
        return nc

    def bc_load(dst, src_ap, src_t):
        kb.dma("sp", dst.t[:], src_ap.partition_broadcast(128), reads=[src_t], writes=[dst])

    def make_pre(G, SH, l, ty, which, tmp):
        base = 3 * which
        bc_load(SH, mods.t[l, ty, (base + 0) * D:(base + 1) * D], mods)
        bc_load(tmp, mods.t[l, ty, (base + 1) * D:(base + 2) * D], mods)
        bc_load(G, nrm["n_mpre" if which == 0 else "n_fpre"].t[l, :], nrm["n_mpre" if which == 0 else "n_fpre"])
        kb.op("dve", lambda: V.scalar_tensor_tensor(out=G.t[:], in0=tmp.t[:], scalar=1.0, in1=G.t[:],
                                                     op0=ALU.add, op1=ALU.mult), reads=[tmp, G], writes=[G])

    def make_post(G, l, ty, which, tmp):
        base = 3 * which
        bc_load(tmp, mods.t[l, ty, (base + 2) * D:(base + 3) * D], mods)
        bc_load(G, nrm["n_mpost" if which == 0 else "n_fpost"].t[l, :], nrm["n_mpost" if which == 0 else "n_fpost"])
        kb.op("dve", lambda: V.tensor_tensor(out=G.t[:], in0=G.t[:], in1=tmp.t[:], op=ALU.mult),
              reads=[tmp, G], writes=[G])

    def prenorm_tile(xt, G, SH, dst, c0, W, halo=None):
        ss, junk, tmp = W["ss"], W["junk"], W["tmp"]
        kb.op("act", lambda: S_.activation(out=junk.t[:], in_=xt.t[:], func=AF.Square, accum_out=ss.t[:, 0:1]),
              reads=[xt], writes=[junk, ss])
        kb.op("act", lambda: S_.activation(out=ss.t[:, 1:2], in_=ss.t[:, 0:1], func=AF.Sqrt, scale=1.0 / D, bias=EPS),
              reads=[ss], writes=[ss])
        kb.op("dve", lambda: V.reciprocal(out=ss.t[:, 2:3], in_=ss.t[:, 1:2]), reads=[ss], writes=[ss])
        kb.op("dve", lambda: V.scalar_tensor_tensor(out=tmp.t[:], in0=xt.t[:], scalar=ss.t[:, 2:3], in1=G.t[:],
                                                     op0=ALU.mult, op1=ALU.mult), reads=[xt, ss, G], writes=[tmp])
        kb.op("dve", lambda: V.tensor_tensor(out=tmp.t[:], in0=tmp.t[:], in1=SH.t[:], op=ALU.add),
              reads=[tmp, SH], writes=[tmp])
        if halo:
            for row, hrow in halo:
                kb.dma("sp", halo_loc.t[hrow:hrow + 1, :], tmp.t[row:row + 1, :], reads=[tmp], writes=[halo_loc])
        kind, hT = dst
        tgt = hT if kind == "sb" else W["hst"]
        for q4 in range(4):
            pp = ps[4 + q4 % 3]
            for i in range(4):
                kc = q4 * 4 + i
                kb.op("pe", lambda kc=kc, i=i, pp=pp: P_.transpose(out=pp.t[:, i * 128:(i + 1) * 128],
                                                                  in_=tmp.t[:, kc * 128:(kc + 1) * 128],
                                                                  identity=ident.t[:]),
                      reads=[tmp, ident], writes=[pp])
            o_ap = (hT.t[:, q4 * 4:(q4 + 1) * 4, c0:c0 + 128] if kind == "sb" else tgt.t[:, q4 * 4:(q4 + 1) * 4, :])
            kb.op("act", lambda pp=pp, o_ap=o_ap: S_.copy(out=o_ap, in_=pp.t[:, :].rearrange("p (a b) -> p a b", a=4)),
                  reads=[pp], writes=[tgt])
        if kind == "dr":
            kb.dma("sp", hT.t[:, :, c0:c0 + 128], tgt.t[:], reads=[tgt], writes=[hT])

    def norm_ws(st):
        return {"ss": sb("n_ss", [128, 4], stack=st), "junk": sb("n_junk", [128, D], BF16, stack=st),
                "tmp": sb("n_tmp", [128, D], stack=st), "hst": sb("n_hst", [128, KC, 128], BF16, stack=st)}

    st_h = contextlib.ExitStack()
    hT = sb("hT0", [128, KC, TT], BF16, stack=st_h)
    with contextlib.ExitStack() as st:
        W = norm_ws(st)
        Gl, SHl, Gc, SHc = [sb(n, [128, D], stack=st) for n in ["Gl", "SHl", "Gc", "SHc"]]
        tmpb = sb("tmpb", [128, D], stack=st)
        make_pre(Gl, SHl, 0, 0, 0, tmpb)
        make_pre(Gc, SHc, 0, 1, 0, tmpb)
        xts = [sb(f"xt{i}", [128, D], stack=st) for i in range(2)]
        for tt in range(18):
            xt = xts[tt % 2]
            if tt < 2:
                kb.dma("sp", xt.t[:], ctx_in.t[tt * 128:(tt + 1) * 128, :], writes=[xt])
                prenorm_tile(xt, Gc, SHc, ("sb", hT), tt * 128, W)
            else:
                kb.dma("sp", xt.t[:], x_in.t[(tt - 2) * 128:(tt - 1) * 128, :], writes=[xt])
                prenorm_tile(xt, Gl, SHl, ("sb", hT), tt * 128, W)
        kb.barrier()
    if stop == "h0":
        hdbg = dram("hdbg", [128, KC, TT], BF16)
        kb.dma("sp", hdbg.t[:, :, :], hT.t[:], reads=[hT], writes=[hdbg])
        kb.barrier()
        return nc

    st_g = contextlib.ExitStack()

    def gsb(name, shape, dt=F32):
        return sb(name, shape, dt, stack=st_g)

    rmask = gsb("rmask", [128, TT], BF16)
    mask8f = gsb("mask8f", [128, 512])
    mask8b = gsb("mask8b", [128, 512])
    kb.dma("pool", rmask.t[:], c_rmask.t[:, :], writes=[rmask])
    kb.dma("sp", mask8f.t[:], c_maskf.t[:, :], writes=[mask8f])
    kb.dma("sp", mask8b.t[:], c_maskb.t[:, :], writes=[mask8b])
    lbt = gsb("lbt", [128, 16, 4])
    lb = gsb("lb", [128, 2, 16])
    oml = gsb("oml", [128, 2, 16])
    st_l4 = contextlib.ExitStack()
    L4 = sb("L4", [4, 2048], stack=st_l4)
    kb.dma("sp", L4.t[:], lbl.t.rearrange("a j n -> (a j) n"), writes=[L4])
    kb.barrier()
    for h_ in range(16):
        kb.op("pe", lambda h_=h_: P_.transpose(out=ps[0].t[:, h_ * 4:(h_ + 1) * 4], in_=L4.t[0:4, h_ * 128:(h_ + 1) * 128],
                                               identity=ident.t[0:4, 0:4]), reads=[L4, ident], writes=[ps[0]])
    kb.op("dve", lambda: V.tensor_copy(out=lbt.t[:], in_=ps[0].t[:, 0:64].rearrange("p (h f) -> p h f", f=4)),
          reads=[ps[0]], writes=[lbt])
    for a in range(2):
        kb.op("dve", lambda a=a: V.tensor_tensor(out=lb.t[:, a, :], in0=lbt.t[:, :, 2 * a], in1=lbt.t[:, :, 2 * a + 1],
                                                 op=ALU.subtract), reads=[lbt], writes=[lb])
    kb.op("act", lambda: S_.activation(out=lb.t[:], in_=lb.t[:], func=AF.Sigmoid), reads=[lb], writes=[lb])
    kb.op("dve", lambda: V.tensor_scalar(out=oml.t[:], in0=lb.t[:], scalar1=-1.0, scalar2=1.0, op0=ALU.mult, op1=ALU.add),
          reads=[lb], writes=[oml])
    kb.barrier()
    st_l4.close()
    sctx = dram("sctx", [2, 16, 128, 128])
    wt = gsb("hg_wt", [128, KC, 5, 128], BF16)
    qs = gsb("hg_qs", [128, TT], BF16)
    sg = gsb("hg_sg", [128, TT], BF16)
    vA = gsb("hg_vA", [128, 18, 128], BF16)
    vB = gsb("hg_vB", [128, 18, 128], BF16)
    kb.op("dve", lambda: V.memset(vA.t[:], 0.0), writes=[vA])
    kb.op("dve", lambda: V.memset(vB.t[:], 0.0), writes=[vB])
    oacc = gsb("hg_oacc", [128, TT])
    bA, bB, bC = gsb("hg_A", [128, TT]), gsb("hg_B", [128, TT]), gsb("hg_C", [128, TT])
    qt = gsb("hg_qt", [128, TT], BF16)
    kt = gsb("hg_kt", [128, TT], BF16)
    ktm = gsb("hg_ktm", [128, 18, 128], BF16)
    eT = gsb("hg_eT", [128, 36])
    Tt = gsb("hg_Tt", [128, 36])
    asum = gsb("hg_asum", [128, 2])
    asum_all = gsb("hg_asum_all", [128, 32])
    aT = gsb("hg_aT", [128, 128])
    arow = gsb("hg_arow", [32, 128])
    Wst = gsb("hg_W", [128, 128])
    Wfin = gsb("hg_Wfin", [128, 128])
    Wbs = [gsb(f"hg_Wb{i}", [128, 128], BF16) for i in range(4)]
    PT8 = gsb("hg_PT8", [128, 512], BF16)
    Sg = gsb("hg_Sg", [128, 4, 128])
    Sg8 = gsb("hg_Sg8", [128, 8, 128])
    Ag = gsb("hg_Ag", [128, 4])
    chn = gsb("hg_chn", [128, 4, 128])

    def proj_fm(g, evac):
        for bi, (c0, n) in enumerate(blocks(TT)):
            pp = ps[bi % 2]
            for kc in range(KC):
                kb.op("pe", lambda kc=kc: P_.matmul(pp.t[:, 0:n], lhsT=wt.t[:, kc, g, :], rhs=hT.t[:, kc, c0:c0 + n],
                                                    start=(kc == 0), stop=(kc == KC - 1)),
                      reads=[wt, hT], writes=[pp])
            evac(pp, c0, n)

    def act_evac(dst, func):
        def f(pp, c0, n):
            kb.op("act", lambda: S_.activation(out=dst.t[:, c0:c0 + n], in_=pp.t[:, 0:n], func=func),
                  reads=[pp], writes=[dst])
        return f

    def proj_v():
        pp = ps[2]
        for t0_ in range(0, 18, 4):
            nt = min(4, 18 - t0_)
            for i in range(nt):
                tt = t0_ + i
                for kc in range(KC):
                    kb.op("pe", lambda kc=kc, i=i, tt=tt: P_.matmul(pp.t[:, i * 128:(i + 1) * 128],
                                                                    lhsT=hT.t[:, kc, tt * 128:(tt + 1) * 128],
                                                                    rhs=wt.t[:, kc, 3, :],
                                                                    start=(kc == 0), stop=(kc == KC - 1)),
                          reads=[wt, hT], writes=[pp])
            kb.op("dve", lambda t0_=t0_, nt=nt: V.tensor_copy(out=vA.t[0:64, t0_:t0_ + nt, :],
                                                              in_=pp.t[0:64, 0:nt * 128].rearrange("p (a b) -> p a b", a=nt)),
                  reads=[pp], writes=[vA])
            kb.op("act", lambda t0_=t0_, nt=nt: S_.copy(out=vB.t[64:128, t0_:t0_ + nt, :],
                                                        in_=pp.t[64:128, 0:nt * 128].rearrange("p (a b) -> p a b", a=nt)),
                  reads=[pp], writes=[vB])

    SEGS = [[0], [1, 2, 3, 4]]

    def blk_chunks(bi):
        return list(range(0, 4)) if bi == 0 else list(range(4 + 8 * (bi - 1), 12 + 8 * (bi - 1)))

    def hg_dir(hd, d, full, s_start):
        proj_fm(1 + d, act_evac(bA, AF.Sigmoid))
        kb.op("dve", lambda: V.tensor_scalar(out=bA.t[:], in0=bA.t[:], scalar1=oml.t[:, d, hd:hd + 1],
                                             scalar2=lb.t[:, d, hd:hd + 1], op0=ALU.mult, op1=ALU.add),
              reads=[bA, oml, lb], writes=[bA])
        kb.op("act", lambda: S_.activation(out=bB.t[:], in_=bA.t[:], func=AF.Ln), reads=[bA], writes=[bB])
        kb.op("dve", lambda: V.tensor_scalar(out=bA.t[:], in0=bA.t[:], scalar1=-1.0, scalar2=1.0,
                                             op0=ALU.mult, op1=ALU.add), reads=[bA], writes=[bA])
        kb.op("dve", lambda: V.tensor_tensor_scan(out=bC.t[:], data0=rmask.t[:], data1=bB.t[:], initial=0.0,
                                                  op0=ALU.mult, op1=ALU.add), reads=[rmask, bB], writes=[bC])
        kb.op("dve", lambda: V.tensor_copy(out=Tt.t[:], in_=bC.t[:, 63:TT:64]), reads=[bC], writes=[Tt])
        kb.op("act", lambda: S_.activation(out=eT.t[:], in_=Tt.t[:], func=AF.Exp), reads=[Tt], writes=[eT])
        if d == 1:
            kb.op("dve", lambda: V.tensor_tensor(out=bC.t[:], in0=bC.t[:], in1=bB.t[:], op=ALU.subtract),
                  reads=[bC, bB], writes=[bC])
        sgn = 1.0 if d == 0 else -1.0
        if full:
            kb.op("act", lambda: S_.activation(out=bB.t[:], in_=bC.t[:], func=AF.Exp, scale=sgn), reads=[bC], writes=[bB])
            kb.op("dve", lambda: V.scalar_tensor_tensor(out=qt.t[:], in0=qs.t[:], scalar=128.0 ** -0.5, in1=bB.t[:],
                                                         op0=ALU.mult, op1=ALU.mult), reads=[qs, bB], writes=[qt])
        kb.op("act", lambda: S_.activation(out=bB.t[:], in_=bC.t[:], func=AF.Exp, scale=-sgn), reads=[bC], writes=[bB])
        kb.op("dve", lambda: V.tensor_tensor(out=kt.t[:], in0=bA.t[:], in1=bB.t[:], op=ALU.mult),
              reads=[bA, bB], writes=[kt])
        for t0_ in range(0, 18, 8):
            nt = min(8, 18 - t0_)
            for i in range(nt):
                tt = t0_ + i
                kb.op("pe", lambda i=i, tt=tt: P_.transpose(out=psb.t[:, i * 128:(i + 1) * 128],
                                                            in_=kt.t[:, tt * 128:(tt + 1) * 128], identity=identb.t[:]),
                      reads=[kt, identb], writes=[psb])
            kb.op("act", lambda t0_=t0_, nt=nt: S_.copy(out=ktm.t[:, t0_:t0_ + nt, :],
                                                        in_=psb.t[:, 0:nt * 128].rearrange("p (a b) -> p a b", a=nt)),
                  reads=[psb], writes=[ktm])
        mask8 = mask8f if d == 0 else mask8b
        wbi = [0]
        for seg in SEGS:
            is_ctx = seg == [0]
            if is_ctx or not full:
                kb.op("dve", lambda: V.memset(Wst.t[:], 0.0), writes=[Wst])
            else:
                kb.op("dve", lambda: V.tensor_copy(out=Wst.t[:], in_=s_start.t[:]), reads=[s_start], writes=[Wst])
            order_b = seg if d == 0 else seg[::-1]
            jprev = None
            for bi in order_b:
                ch = blk_chunks(bi)
                ch_o = ch if d == 0 else ch[::-1]
                c0 = ch[0] * 64
                nb = len(ch)
                for i, j in enumerate(ch):
                    pD = ps[3 + i // 4]
                    vX = vA if j % 2 == 0 else vB
                    kb.op("pe", lambda i=i, j=j, pD=pD, vX=vX: P_.matmul(pD.t[:, (i % 4) * 128:(i % 4 + 1) * 128],
                                                                         lhsT=ktm.t[:, j // 2, :], rhs=vX.t[:, j // 2, :],
                                                                         start=True, stop=True),
                          reads=[ktm, vX], writes=[pD])
                if full:
                    for i, j in enumerate(ch):
                        kb.op("pe", lambda i=i, j=j: P_.matmul(ps[5].t[:, i * 64:(i + 1) * 64],
                                                               lhsT=kt.t[:, (j // 2) * 128:(j // 2 + 1) * 128],
                                                               rhs=qt.t[:, j * 64:(j + 1) * 64], start=True, stop=True),
                              reads=[kt, qt], writes=[ps[5]])
                    kb.op("dve", lambda nb=nb: V.tensor_tensor(out=PT8.t[:, 0:nb * 64], in0=ps[5].t[:, 0:nb * 64],
                                                               in1=mask8.t[:, 0:nb * 64], op=ALU.mult),
                          reads=[ps[5], mask8], writes=[PT8])
                for j in ch_o:
                    i = j - ch[0]
                    pD = ps[3 + i // 4]
                    if d == 0:
                        e_ap = eT.t[:, jprev:jprev + 1] if jprev is not None else ones_f.t[:, 0:1]
                    else:
                        e_ap = eT.t[:, j:j + 1]
                    if full:
                        Wb = Wbs[wbi[0] % 4]
                        wbi[0] += 1
                        kb.op("dve", lambda Wb=Wb, e_ap=e_ap: V.tensor_scalar(out=Wb.t[:], in0=Wst.t[:], scalar1=e_ap,
                                                                               scalar2=None, op0=ALU.mult),
                              reads=[Wst, eT], writes=[Wb])
                        vX = vA if j % 2 == 0 else vB
                        kb.op("pe", lambda i=i, j=j, vX=vX: P_.matmul(ps[6].t[:, i * 64:(i + 1) * 64], lhsT=vX.t[:, j // 2, :],
                                                                      rhs=PT8.t[:, i * 64:(i + 1) * 64], start=True, stop=False),
                              reads=[vX, PT8], writes=[ps[6]])
                        kb.op("pe", lambda i=i, j=j, Wb=Wb: P_.matmul(ps[6].t[:, i * 64:(i + 1) * 64], lhsT=Wb.t[:],
                                                                      rhs=qt.t[:, j * 64:(j + 1) * 64],
                                                                      start=False, stop=True),
                              reads=[Wb, qt], writes=[ps[6]])
                    kb.op("dve", lambda i=i, pD=pD, e_ap=e_ap: V.scalar_tensor_tensor(
                        out=Wst.t[:], in0=Wst.t[:], scalar=e_ap, in1=pD.t[:, (i % 4) * 128:(i % 4 + 1) * 128],
                        op0=ALU.mult, op1=ALU.add), reads=[Wst, eT, pD], writes=[Wst])
                    jprev = j
                if full:
                    if d == 0:
                        kb.op("act", lambda c0=c0, nb=nb: S_.copy(out=oacc.t[:, c0:c0 + nb * 64], in_=ps[6].t[:, 0:nb * 64]),
                              reads=[ps[6]], writes=[oacc])
                    else:
                        kb.op("dve", lambda c0=c0, nb=nb: V.tensor_tensor(out=oacc.t[:, c0:c0 + nb * 64],
                                                                          in0=oacc.t[:, c0:c0 + nb * 64],
                                                                          in1=ps[6].t[:, 0:nb * 64], op=ALU.add),
                              reads=[ps[6], oacc], writes=[oacc])
            if not full:
                if d == 0:
                    jl = jprev
                    kb.op("dve", lambda jl=jl: V.tensor_scalar(out=Wfin.t[:], in0=Wst.t[:], scalar1=eT.t[:, jl:jl + 1],
                                                               scalar2=None, op0=ALU.mult), reads=[Wst, eT], writes=[Wfin])
                else:
                    kb.op("dve", lambda: V.tensor_copy(out=Wfin.t[:], in_=Wst.t[:]), reads=[Wst], writes=[Wfin])
                if is_ctx:
                    kb.dma("sp", sctx.t[d, hd], Wfin.t[:], reads=[Wfin], writes=[sctx])
                else:
                    r0 = (d * 16 + hd) * 128
                    kb.dma("sp", st_loc.t[r0:r0 + 128, :], Wfin.t[:], reads=[Wfin], writes=[st_loc])
                    kb.op("dve", lambda: V.reduce_sum(out=asum.t[:, 0:1], in_=Tt.t[:, 4:36], axis=AX.X),
                          reads=[Tt], writes=[asum])
                    kb.op("act", lambda: S_.activation(out=asum_all.t[:, d * 16 + hd:d * 16 + hd + 1], in_=asum.t[:, 0:1],
                                                       func=AF.Exp), reads=[asum], writes=[asum_all])

    def load_wt(hd):
        kb.dma("pool", wt.t[:].rearrange("p k g c -> p (k g c)"), wrows("hgwin", hd * 128, 128),
               reads=[WF["hgwin"]], writes=[wt])

    for hd in range(16):
        load_wt(hd)
        proj_v()
        for d in range(2):
            hg_dir(hd, d, False, None)
    kb.barrier()
    kb.op("pe", lambda: P_.transpose(out=ps[0].t[0:32, 0:128], in_=asum_all.t[:, :], identity=ident.t[:]),
          reads=[asum_all, ident], writes=[ps[0]])
    kb.op("dve", lambda: V.tensor_copy(out=arow.t[:], in_=ps[0].t[0:32, 0:128]), reads=[ps[0]], writes=[arow])
    kb.barrier()
    kb.dma("sp", st_loc.t[4096:4128, :], arow.t[:], reads=[arow], writes=[st_loc])
    kb.collective([st_loc.t.opt()], [st_all.t.opt()], G8, reads=[st_loc], writes=[st_all])
    if stop == "hg1":
        kb.barrier()
        return nc

    st3 = st_all.t.rearrange("(r q) v -> r q v", r=8)
    arow4 = gsb("hg_arow4", [128, 2, 128])
    for rk in range(8):
        kb.dma("sp", arow4.t[(rk % 4) * 32:(rk % 4 + 1) * 32, rk // 4, :], st3[rk, 4096:4128, :], reads=[st_all], writes=[arow4])
    for b_ in range(2):
        kb.op("pe", lambda b_=b_: P_.transpose(out=ps[0].t[:, b_ * 128:(b_ + 1) * 128], in_=arow4.t[:, b_, :], identity=ident.t[:]),
              reads=[arow4, ident], writes=[ps[0]])
    kb.op("dve", lambda: V.tensor_scalar(out=aT.t[:], in0=ps[0].t[:, 0:128], scalar1=eb.t[:, 0:1], scalar2=None, op0=ALU.mult),
          reads=[ps[0], eb], writes=[aT])
    kb.op("dve", lambda: V.scalar_tensor_tensor(out=aT.t[:], in0=ps[0].t[:, 128:256], scalar=eb.t[:, 1:2], in1=aT.t[:],
                                                 op0=ALU.mult, op1=ALU.add), reads=[ps[0], eb, aT], writes=[aT])
    s_fw = gsb("hg_sfw", [128, 128])
    s_bw = gsb("hg_sbw", [128, 128])
    for hd in range(16):
        load_wt(hd)
        proj_fm(0, act_evac(qs, AF.Silu))
        proj_fm(4, act_evac(sg, AF.Silu))
        proj_v()
        for d in range(2):
            q_ = d * 16 + hd
            kb.dma("sp", Sg8.t[:], st3[:, q_ * 128:(q_ + 1) * 128, :].rearrange("r p v -> p r v"), reads=[st_all], writes=[Sg8])
            kb.op("dve", lambda: V.tensor_scalar(out=Sg.t[:], in0=Sg8.t[:, 0:4, :], scalar1=eb.t[:, 0:1], scalar2=None,
                                                 op0=ALU.mult), reads=[Sg8, eb], writes=[Sg])
            kb.op("dve", lambda: V.scalar_tensor_tensor(out=Sg.t[:], in0=Sg8.t[:, 4:8, :], scalar=eb.t[:, 1:2], in1=Sg.t[:],
                                                         op0=ALU.mult, op1=ALU.add), reads=[Sg8, eb, Sg], writes=[Sg])
            kb.op("dve", lambda q_=q_: V.tensor_copy(out=Ag.t[:], in_=aT.t[:, q_:128:32]), reads=[aT], writes=[Ag])
            dst = s_fw if d == 0 else s_bw
            order = [0, 1, 2, 3] if d == 0 else [3, 2, 1, 0]
            kb.dma("sp", chn.t[:, order[0], :], sctx.t[d, hd], reads=[sctx], writes=[chn])
            for a_, b_ in zip(order[:-1], order[1:]):
                kb.op("dve", lambda a_=a_, b_=b_: V.scalar_tensor_tensor(out=chn.t[:, b_, :], in0=chn.t[:, a_, :],
                                                                         scalar=Ag.t[:, a_:a_ + 1], in1=Sg.t[:, a_, :],
                                                                         op0=ALU.mult, op1=ALU.add),
                      reads=[chn, Ag, Sg], writes=[chn])
            kb.op("dve", lambda: V.tensor_scalar(out=dst.t[:], in0=chn.t[:, 0, :], scalar1=e4.t[:, 0:1], scalar2=None,
                                                 op0=ALU.mult), reads=[chn, e4], writes=[dst])
            for r in range(1, 4):
                kb.op("dve", lambda r=r: V.scalar_tensor_tensor(out=dst.t[:], in0=chn.t[:, r, :], scalar=e4.t[:, r:r + 1],
                                                                in1=dst.t[:], op0=ALU.mult, op1=ALU.add),
                      reads=[chn, e4, dst], writes=[dst])
            hg_dir(hd, d, True, dst)
        for bi, (c0, n) in enumerate(blocks(TT)):
            pp = ps[bi % 2]
            kb.op("act", lambda c0=c0, n=n: S_.activation(out=bB.t[:, c0:c0 + n], in_=oacc.t[:, c0:c0 + n], func=AF.Square),
                  reads=[oacc], writes=[bB])
            kb.op("pe", lambda c0=c0, n=n, pp=pp: P_.matmul(pp.t[:, 0:n], lhsT=ones_f.t[:], rhs=bB.t[:, c0:c0 + n],
                                                            start=True, stop=True), reads=[ones_f, bB], writes=[pp])
            kb.op("act", lambda c0=c0, n=n, pp=pp: S_.activation(out=bC.t[:, c0:c0 + n], in_=pp.t[:, 0:n], func=AF.Sqrt,
                                                                 scale=1.0 / 128, bias=EPS), reads=[pp], writes=[bC])
        kb.op("dve", lambda: V.reciprocal(out=bC.t[:], in_=bC.t[:]), reads=[bC], writes=[bC])
        kb.op("dve", lambda: V.scalar_tensor_tensor(out=bA.t[:], in0=oacc.t[:], scalar=gvec.t[:, 0:1], in1=bC.t[:],
                                                     op0=ALU.mult, op1=ALU.mult), reads=[oacc, gvec, bC], writes=[bA])
        kb.op("dve", lambda: V.tensor_tensor(out=qt.t[:], in0=bA.t[:], in1=sg.t[:], op=ALU.mult),
              reads=[bA, sg], writes=[qt])
        kb.dma("sp", zT.t.rearrange("n p k t -> p n k t")[:, :, hd, :], qt.t[:].rearrange("p (n t) -> p n t", t=128),
               reads=[qt], writes=[zT])
    kb.barrier()
    st_g.close()
    st_h.close()
    if stop == "hg":
        kb.barrier()
        return nc

    zTv = zT.t.rearrange("n p k t -> p n k t")
    hidv = hid.t.rearrange("n p k t -> p n k t")

    def outproj_phase(wname, KCH, zsrc, tiles, nsplit, post, nxt):
        DW = D // nsplit
        nb = DW // 512
        with contextlib.ExitStack() as st:
            W = norm_ws(st)
            types = sorted(set(t_["ty"] for t_ in tiles))
            tmpb = sb("op_tmpb", [128, D], stack=st)
            Gp, Gn, SHn = {}, {}, {}
            for ty in types:
                Gp[ty] = sb(f"op_Gp{ty}", [128, D], stack=st)
                make_post(Gp[ty], post[0], ty, post[1], tmpb)
                if nxt:
                    Gn[ty] = sb(f"op_Gn{ty}", [128, D], stack=st)
                    SHn[ty] = sb(f"op_SHn{ty}", [128, D], stack=st)
                    make_pre(Gn[ty], SHn[ty], nxt[0], ty, nxt[1], tmpb)
            wres = sb("op_w", [128, KCH, DW], BF16, stack=st)
            zts = [sb(f"op_z{i}", [128, KCH, 128], BF16, stack=st) for i in range(2)]
            yt = sb("op_y", [128, D], stack=st)
            xo = sb("op_xo", [128, D], stack=st)
            xn = sb("op_xn", [128, D], stack=st)
            ss2 = sb("op_ss", [128, 4], stack=st)
            for sp in range(nsplit):
                for kc in range(KCH):
                    kb.dma("pool", wres.t[:, kc, :], wrows(wname, kc * 128, 128)[:, sp * DW:(sp + 1) * DW],
                           reads=[WF[wname]], writes=[wres])
                for ti, tl in enumerate(tiles):
                    zt = zts[ti % 2]
                    kb.dma("sp", zt.t[:], zsrc.t[tl["zi"]], reads=[zsrc], writes=[zt])
                    for db in range(nb):
                        for kc in range(KCH):
                            kb.op("pe", lambda kc=kc, db=db, zt=zt: P_.matmul(ps[db].t[:, :], lhsT=zt.t[:, kc, :],
                                                                              rhs=wres.t[:, kc, db * 512:(db + 1) * 512],
                                                                              start=(kc == 0), stop=(kc == KCH - 1)),
                                  reads=[zt, wres], writes=[ps[db]])
                        o0 = sp * DW + db * 512
                        kb.op("act", lambda db=db, o0=o0: S_.copy(out=yt.t[:, o0:o0 + 512], in_=ps[db].t[:, :]),
                              reads=[ps[db]], writes=[yt])
                    r0 = ti * 128
                    if sp < nsplit - 1:
                        kb.dma("sp", ystash.t[r0:r0 + 128, sp * DW:(sp + 1) * DW], yt.t[:, sp * DW:(sp + 1) * DW],
                               reads=[yt], writes=[ystash])
                        continue
                    if nsplit > 1:
                        kb.dma("sp", yt.t[:, 0:(nsplit - 1) * DW], ystash.t[r0:r0 + 128, 0:(nsplit - 1) * DW],
                               reads=[ystash], writes=[yt])
                    ty = tl["ty"]
                    kb.op("act", lambda: S_.activation(out=W["junk"].t[:], in_=yt.t[:], func=AF.Square,
                                                       accum_out=ss2.t[:, 0:1]), reads=[yt], writes=[W["junk"], ss2])
                    kb.op("act", lambda: S_.activation(out=ss2.t[:, 1:2], in_=ss2.t[:, 0:1], func=AF.Sqrt,
                                                       scale=1.0 / D, bias=EPS), reads=[ss2], writes=[ss2])
                    kb.op("dve", lambda: V.reciprocal(out=ss2.t[:, 2:3], in_=ss2.t[:, 1:2]), reads=[ss2], writes=[ss2])
                    xs, xr = tl["xold"]
                    kb.dma("sp", xo.t[:], xs.t[xr:xr + 128, :], reads=[xs], writes=[xo])
                    kb.op("dve", lambda ty=ty: V.scalar_tensor_tensor(out=yt.t[:], in0=yt.t[:], scalar=ss2.t[:, 2:3],
                                                                      in1=Gp[ty].t[:], op0=ALU.mult, op1=ALU.mult),
                          reads=[yt, ss2, Gp[ty]], writes=[yt])
                    kb.op("dve", lambda: V.tensor_tensor(out=xn.t[:], in0=yt.t[:], in1=xo.t[:], op=ALU.add),
                          reads=[yt, xo], writes=[xn])
                    xd, xdr = tl["xnew"]
                    kb.dma("sp", xd.t[xdr:xdr + 128, :], xn.t[:], reads=[xn], writes=[xd])
                    if nxt:
                        prenorm_tile(xn, Gn[ty], SHn[ty], ("dr", nxt[2]), tl["hcol"], W, halo=tl.get("halo"))
            kb.barrier()

    def ffn_a(l, hTd, NC, segs, halo_cols, zero_cols):
        NO = NC - 2
        with contextlib.ExitStack() as st:
            h2 = sb("fa_h", [128, KC, NC], BF16, stack=st)
            for kc0 in range(0, KC, 4):
                for (o0, ntok, tile0) in segs:
                    kb.dma("sp", h2.t[:, kc0:kc0 + 4, 1 + o0:1 + o0 + ntok], hTd.t[:, kc0:kc0 + 4, 1 + o0:1 + o0 + ntok],
                           reads=[hTd], writes=[h2])
            for zc in zero_cols:
                kb.op("dve", lambda zc=zc: V.memset(h2.t[:, :, zc:zc + 1], 0.0), writes=[h2])
            hal = sb("fa_hal", [16, D], stack=st)
            hsel = sb("fa_hsel", [16, 2], stack=st)
            kb.collective([halo_loc.t.opt()], [halo_all.t.opt()], G8, reads=[halo_loc], writes=[halo_all])
            kb.dma("sp", hal.t[:], halo_all.t[:, :], reads=[halo_all], writes=[hal])
            kb.dma("sp", hsel.t[:], c_hsel.t[:, :], writes=[hsel])
            for kc in range(KC):
                kb.op("pe", lambda kc=kc: P_.matmul(ps[0].t[:, kc * 2:(kc + 1) * 2], lhsT=hal.t[0:16, kc * 128:(kc + 1) * 128],
                                                    rhs=hsel.t[0:16, :], start=True, stop=True),
                      reads=[hal, hsel], writes=[ps[0]])
            for i, hc in enumerate(halo_cols):
                kb.op("dve", lambda i=i, hc=hc: V.tensor_copy(out=h2.t[:, :, hc:hc + 1],
                                                              in_=ps[0].t[:, 0:32].rearrange("p (k two) -> p k two", two=2)[:, :, i:i + 1]),
                      reads=[ps[0]], writes=[h2])
            cw = sb("fa_cw", [128, FT, 4], stack=st)
            C4 = sb("fa_C4", [4, DFF], stack=st)
            kb.dma("sp", C4.t[0:3, :], convw.t[l, :, :], writes=[C4])
            kb.dma("sp", C4.t[3:4, :], convb.t[l:l + 1, :], writes=[C4])
            for f_ in range(FT):
                kb.op("pe", lambda f_=f_: P_.transpose(out=ps[1].t[:, f_ * 4:(f_ + 1) * 4], in_=C4.t[0:4, f_ * 128:(f_ + 1) * 128],
                                                       identity=ident.t[0:4, 0:4]), reads=[C4, ident], writes=[ps[1]])
            kb.op("dve", lambda: V.tensor_copy(out=cw.t[:], in_=ps[1].t[:, 0:FT * 4].rearrange("p (h f) -> p h f", f=4)),
                  reads=[ps[1]], writes=[cw])
            kb.barrier()
            wgu = [sb(f"fa_wgu{i}", [128, KC, 256], BF16, stack=st) for i in range(2)]
            Gs = sb("fa_G", [128, NC], stack=st)
            t1 = sb("fa_t1", [128, NC], stack=st)
            sl = sb("fa_sl", [128, NC], stack=st)
            hds = [sb(f"fa_hid{i}", [128, NC], BF16, stack=st) for i in range(2)]
            for ft in range(FT):
                g_ = u_ = wgu[ft % 2]
                hd_ = hds[ft % 2]
                kb.dma("pool", g_.t[:].rearrange("p k c -> p (k c)"), wrows(f"fin{l}", ft * 128, 128),
                       reads=[WF[f"fin{l}"]], writes=[g_])
                for bi, (c0, n) in enumerate(eblocks(NC)):
                    pp = ps[bi % 2]
                    for kc in range(KC):
                        kb.op("pe", lambda kc=kc, pp=pp, c0=c0, n=n: P_.matmul(pp.t[:, 0:n], lhsT=g_.t[:, kc, 0:128],
                                                                               rhs=h2.t[:, kc, c0:c0 + n],
                                                                               start=(kc == 0), stop=(kc == KC - 1)),
                              reads=[g_, h2], writes=[pp])
                    kb.op("act", lambda pp=pp, c0=c0, n=n: S_.copy(out=Gs.t[:, c0:c0 + n], in_=pp.t[:, 0:n]),
                          reads=[pp], writes=[Gs])
                kb.op("dve", lambda: V.tensor_scalar(out=t1.t[:, 0:NO], in0=Gs.t[:, 1:NC - 1], scalar1=cw.t[:, ft, 1:2],
                                                     scalar2=cw.t[:, ft, 3:4], op0=ALU.mult, op1=ALU.add),
                      reads=[Gs, cw], writes=[t1])
                kb.op("dve", lambda: V.scalar_tensor_tensor(out=t1.t[:, 0:NO], in0=Gs.t[:, 0:NO], scalar=cw.t[:, ft, 0:1],
                                                             in1=t1.t[:, 0:NO], op0=ALU.mult, op1=ALU.add),
                      reads=[Gs, cw, t1], writes=[t1])
                kb.op("dve", lambda: V.scalar_tensor_tensor(out=t1.t[:, 0:NO], in0=Gs.t[:, 2:NC], scalar=cw.t[:, ft, 2:3],
                                                             in1=t1.t[:, 0:NO], op0=ALU.mult, op1=ALU.add),
                      reads=[Gs, cw, t1], writes=[t1])
                kb.op("act", lambda: S_.activation(out=sl.t[:, 0:NO], in_=t1.t[:, 0:NO], func=AF.Silu),
                      reads=[t1], writes=[sl])
                for bi, (c0, n) in enumerate(eblocks(NO)):
                    pp = ps[2 + bi % 2]
                    for kc in range(KC):
                        kb.op("pe", lambda kc=kc, pp=pp, c0=c0, n=n: P_.matmul(pp.t[:, 0:n], lhsT=u_.t[:, kc, 128:256],
                                                                               rhs=h2.t[:, kc, 1 + c0:1 + c0 + n],
                                                                               start=(kc == 0), stop=(kc == KC - 1)),
                              reads=[u_, h2], writes=[pp])
                    kb.op("dve", lambda pp=pp, c0=c0, n=n: V.tensor_tensor(out=hd_.t[:, c0:c0 + n], in0=sl.t[:, c0:c0 + n],
                                                                           in1=pp.t[:, 0:n], op=ALU.mult),
                          reads=[sl, pp], writes=[hd_])
                for (o0, ntok, tile0) in segs:
                    kb.dma("sp", hidv[:, tile0:tile0 + ntok // 128, ft, :],
                           hd_.t[:, o0:o0 + ntok].rearrange("p (n t) -> p n t", t=128), reads=[hd_], writes=[hid])
            kb.barrier()

    tiles = []
    for tt in range(18):
        if tt < 2:
            tiles.append(dict(zi=tt, ty=1, xold=(ctx_in, tt * 128), xnew=(x1, tt * 128), hcol=1 + tt * 128))
        else:
            tl = dict(zi=tt, ty=0, xold=(x_in, (tt - 2) * 128), xnew=(x1, tt * 128), hcol=259 + (tt - 2) * 128)
            if tt == 2:
                tl["halo"] = [(0, 0)]
            if tt == 17:
                tl["halo"] = [(127, 1)]
            tiles.append(tl)
    outproj_phase("hgwout", KC, zT, tiles, 1, (0, 0), (0, 1, hTd2))
    if stop == "x1":
        kb.barrier()
        return nc
    ffn_a(0, hTd2, TT + 4, [(0, 256, 0), (258, 2048, 2)], [258, 2307], [0, 257])
    tiles = [dict(zi=tt, ty=(1 if tt < 2 else 0), xold=(x1, tt * 128), xnew=(x2, tt * 128), hcol=tt * 128)
             for tt in range(18)]
    outproj_phase("fout0", FT, hid, tiles, 4, (0, 1), (1, 0, hTd3))
    if stop == "x2":
        kb.barrier()
        return nc

    with contextlib.ExitStack() as st:
        h3 = sb("at_h", [128, KC, TT], BF16, stack=st)
        for kc0 in range(0, KC, 4):
            kb.dma("sp", h3.t[:, kc0:kc0 + 4, :], hTd3.t[:, kc0:kc0 + 4, :], reads=[hTd3], writes=[h3])
        qTall = sb("at_q", [128, 16, TL], BF16, stack=st)
        with contextlib.ExitStack() as st2:
            ropeC = sb("at_rc", [128, TL], stack=st2)
            ropeS = sb("at_rs", [128, TL], stack=st2)
            perm = sb("at_perm", [128, 128], stack=st2)
            kb.dma("sp", ropeC.t[:], c_ropec.t[:, :], writes=[ropeC])
            kb.dma("sp", ropeS.t[:], c_ropes.t[:, :], writes=[ropeS])
            kb.dma("sp", perm.t[:], c_perm.t[:, :], writes=[perm])
            wq = [sb(f"at_wq{i}", [128, KC, 128], BF16, stack=st2) for i in range(2)]
            wvv = sb("at_wv", [128, KC, 512], BF16, stack=st2)
            qr, sq, rs, qn, t1, t2 = [sb(f"at_{n}", [128, 512], stack=st2) for n in ["qr", "sq", "rs", "qn", "t1", "t2"]]
            kst = sb("at_kst", [128, TT], BF16, stack=st2)
            vt = [sb(f"at_vt{i}", [128, 512], BF16, stack=st2) for i in range(2)]

            def qk_head(idx, wcol, gcol, is_q, dst_fn):
                w_ = wq[idx % 2]
                kb.dma("pool", w_.t[:].rearrange("p k c -> p (k c)"), wrows("qkv", wcol, 128),
                       reads=[WF["qkv"]], writes=[w_])
                blks = ([] if is_q else [(0, TC, False)]) + [(TC + i * 512, 512, True) for i in range(4)]
                for bi, (c0, n, rope) in enumerate(blks):
                    pp = ps[bi % 2]
                    for kc in range(KC):
                        kb.op("pe", lambda kc=kc, pp=pp: P_.matmul(pp.t[:, 0:n], lhsT=w_.t[:, kc, :], rhs=h3.t[:, kc, c0:c0 + n],
                                                                   start=(kc == 0), stop=(kc == KC - 1)),
                              reads=[w_, h3], writes=[pp])
                    kb.op("act", lambda pp=pp: S_.copy(out=qr.t[:, 0:n], in_=pp.t[:, 0:n]), reads=[pp], writes=[qr])
                    kb.op("act", lambda pp=pp: S_.activation(out=sq.t[:, 0:n], in_=pp.t[:, 0:n], func=AF.Square),
                          reads=[pp], writes=[sq])
                    kb.op("pe", lambda: P_.matmul(ps[2].t[:, 0:n], lhsT=ones_f.t[:], rhs=sq.t[:, 0:n], start=True, stop=True),
                          reads=[ones_f, sq], writes=[ps[2]])
                    kb.op("act", lambda: S_.activation(out=rs.t[:, 0:n], in_=ps[2].t[:, 0:n], func=AF.Sqrt,
                                                       scale=1.0 / 128, bias=EPS), reads=[ps[2]], writes=[rs])
                    kb.op("dve", lambda: V.reciprocal(out=rs.t[:, 0:n], in_=rs.t[:, 0:n]), reads=[rs], writes=[rs])
                    kb.op("dve", lambda: V.scalar_tensor_tensor(out=qn.t[:, 0:n], in0=qr.t[:, 0:n],
                                                                 scalar=gvec.t[:, gcol:gcol + 1], in1=rs.t[:, 0:n],
                                                                 op0=ALU.mult, op1=ALU.mult), reads=[qr, gvec, rs], writes=[qn])
                    dst_ap, dst_t = dst_fn(c0, n)
                    if rope:
                        tc0 = c0 - TC
                        kb.op("pe", lambda: P_.matmul(ps[3].t[:, 0:n], lhsT=perm.t[:], rhs=qn.t[:, 0:n], start=True, stop=True),
                              reads=[perm, qn], writes=[ps[3]])
                        kb.op("dve", lambda: V.tensor_tensor(out=t1.t[:, 0:n], in0=qn.t[:, 0:n], in1=ropeC.t[:, tc0:tc0 + n],
                                                             op=ALU.mult), reads=[qn, ropeC], writes=[t1])
                        kb.op("dve", lambda: V.tensor_tensor(out=t2.t[:, 0:n], in0=ps[3].t[:, 0:n], in1=ropeS.t[:, tc0:tc0 + n],
                                                             op=ALU.mult), reads=[ps[3], ropeS], writes=[t2])
                        kb.op("dve", lambda: V.tensor_tensor(out=dst_ap, in0=t1.t[:, 0:n], in1=t2.t[:, 0:n], op=ALU.add),
                              reads=[t1, t2], writes=[dst_t])
                    else:
                        kb.op("dve", lambda: V.tensor_copy(out=dst_ap, in_=qn.t[:, 0:n]), reads=[qn], writes=[dst_t])

            for h in range(16):
                qk_head(h, h * 128, 1, True, lambda c0, n, h=h: (qTall.t[:, h, c0 - TC:c0 - TC + n], qTall))
            for g in range(4):
                qk_head(g, 2048 + g * 128, 2, False, lambda c0, n: (kst.t[:, c0:c0 + n], kst))
                kb.dma("sp", k_ctx.t[g * 128:(g + 1) * 128, :], kst.t[:, 0:TC], reads=[kst], writes=[k_ctx])
                kb.dma("sp", k_loc_v[g * 128:(g + 1) * 128, :], kst.t[:, TC:TT], reads=[kst], writes=[kv_loc])
            for i in range(4):
                kb.dma("pool", wvv.t[:, :, i * 128:(i + 1) * 128],
                       wrows("qkv", (20 + i) * 128, 128).rearrange("p (k c) -> p k c", c=128),
                       reads=[WF["qkv"]], writes=[wvv])
            for tt in range(18):
                pp = ps[tt % 2]
                v_ = vt[tt % 2]
                for kc in range(KC):
                    kb.op("pe", lambda kc=kc, pp=pp, tt=tt: P_.matmul(pp.t[:, :], lhsT=h3.t[:, kc, tt * 128:(tt + 1) * 128],
                                                                      rhs=wvv.t[:, kc, :], start=(kc == 0), stop=(kc == KC - 1)),
                          reads=[h3, wvv], writes=[pp])
                kb.op("act", lambda pp=pp, v_=v_: S_.copy(out=v_.t[:], in_=pp.t[:, :]), reads=[pp], writes=[v_])
                if tt < 2:
                    kb.dma("sp", v_ctx.t[tt * 128:(tt + 1) * 128, :], v_.t[:], reads=[v_], writes=[v_ctx])
                else:
                    kb.dma("sp", v_loc_v[(tt - 2) * 128:(tt - 1) * 128, :], v_.t[:], reads=[v_], writes=[kv_loc])
            kb.collective([kv_loc.t.opt()], [kv_all.t.opt()], G8, reads=[kv_loc], writes=[kv_all])
            kb.barrier()
        NK = TC + 4 * TL
        NST = NK // 128
        kTg = sb("at_kT", [128, NK], BF16, stack=st)
        vg = sb("at_vg", [128, NST, 128], BF16, stack=st)
        Pb = [sb(f"at_P{i}", [128, 512], BF16, stack=st) for i in range(3)]
        rd = sb("at_rd", [128, 512], stack=st)
        ksel = [sb(f"at_ksel{i}", [128, TL], BF16, stack=st) for i in range(2)]
        vsel = [sb(f"at_vsel{i}", [128, 16, 128], BF16, stack=st) for i in range(2)]
        ot = [sb(f"at_ot{i}", [128, 512], BF16, stack=st) for i in range(2)]
        cnt = 0
        for g in range(4):
            kb.dma("sp", kTg.t[:, 0:TC], k_ctx.t[g * 128:(g + 1) * 128, :], reads=[k_ctx], writes=[kTg])
            kb.dma("sp", vg.t[:, 0:2, :], v_ctx.t[:, g * 128:(g + 1) * 128].rearrange("(n p) c -> p n c", p=128),
                   reads=[v_ctx], writes=[vg])
            for r in range(4):
                for b_ in range(2):
                    rr = 4 * b_ + r
                    kb.dma("sp", ksel[b_].t[:], k_all_v(rr)[g * 128:(g + 1) * 128, :],
                           reads=[kv_all], writes=[ksel[b_]])
                    kb.dma("sp", vsel[b_].t[:],
                           v_all_v(rr)[:, g * 128:(g + 1) * 128].rearrange("(n p) c -> p n c", p=128),
                           reads=[kv_all], writes=[vsel[b_]])
                ko = kTg.t[:, TC + r * TL:TC + (r + 1) * TL]
                vo = vg.t[:, 2 + r * 16:2 + (r + 1) * 16, :]
                kb.op("dve", lambda ko=ko: V.tensor_scalar(out=ko, in0=ksel[0].t[:], scalar1=eb.t[:, 0:1], scalar2=None,
                                                           op0=ALU.mult), reads=[ksel[0], eb], writes=[kTg])
                kb.op("dve", lambda ko=ko: V.scalar_tensor_tensor(out=ko, in0=ksel[1].t[:], scalar=eb.t[:, 1:2], in1=ko,
                                                                  op0=ALU.mult, op1=ALU.add), reads=[ksel[1], eb, kTg], writes=[kTg])
                kb.op("dve", lambda vo=vo: V.tensor_scalar(out=vo, in0=vsel[0].t[:], scalar1=eb.t[:, 0:1], scalar2=None,
                                                           op0=ALU.mult), reads=[vsel[0], eb], writes=[vg])
                kb.op("dve", lambda vo=vo: V.scalar_tensor_tensor(out=vo, in0=vsel[1].t[:], scalar=eb.t[:, 1:2], in1=vo,
                                                                  op0=ALU.mult, op1=ALU.add), reads=[vsel[1], eb, vg], writes=[vg])
            for hq in range(4):
                h = 4 * g + hq
                for qb in range(4):
                    pO = ps[3 + 2 * (cnt % 2)] if False else (ps[3] if cnt % 2 == 0 else ps[5])
                    pDn = ps[4] if cnt % 2 == 0 else ps[6]
                    o_ = ot[cnt % 2]
                    cnt += 1
                    for s_ in range(NST):
                        pS = ps[s_ % 3]
                        P = Pb[s_ % 3]
                        kb.op("pe", lambda s_=s_, pS=pS: P_.matmul(pS.t[:, :], lhsT=kTg.t[:, s_ * 128:(s_ + 1) * 128],
                                                                   rhs=qTall.t[:, h, qb * 512:(qb + 1) * 512], start=True, stop=True),
                              reads=[kTg, qTall], writes=[pS])
                        kb.op("act", lambda pS=pS, P=P: S_.activation(out=P.t[:], in_=pS.t[:, :], func=AF.Exp, scale=128.0 ** -0.5),
                              reads=[pS], writes=[P])
                        kb.op("pe", lambda s_=s_, P=P, pO=pO: P_.matmul(pO.t[:, :], lhsT=vg.t[:, s_, :], rhs=P.t[:],
                                                                        start=(s_ == 0), stop=(s_ == NST - 1)),
                              reads=[vg, P], writes=[pO])
                        kb.op("pe", lambda s_=s_, P=P, pDn=pDn: P_.matmul(pDn.t[:, :], lhsT=ones_b.t[:], rhs=P.t[:],
                                                                          start=(s_ == 0), stop=(s_ == NST - 1)),
                              reads=[ones_b, P], writes=[pDn])
                    kb.op("dve", lambda pDn=pDn: V.reciprocal(out=rd.t[:], in_=pDn.t[:, :]), reads=[pDn], writes=[rd])
                    kb.op("dve", lambda pO=pO, o_=o_: V.tensor_tensor(out=o_.t[:], in0=pO.t[:, :], in1=rd.t[:], op=ALU.mult),
                          reads=[pO, rd], writes=[o_])
                    kb.dma("sp", zTv[:, qb * 4:(qb + 1) * 4, h, :], o_.t[:].rearrange("p (n t) -> p n t", t=128),
                           reads=[o_], writes=[zT])
        kb.barrier()
    if stop == "att":
        kb.barrier()
        return nc
    tiles = []
    for ti in range(16):
        tl = dict(zi=ti, ty=0, xold=(x2, TC + ti * 128), xnew=(x3, ti * 128), hcol=1 + ti * 128)
        if ti == 0:
            tl["halo"] = [(0, 0)]
        if ti == 15:
            tl["halo"] = [(127, 1)]
        tiles.append(tl)
    outproj_phase("aout", KC, zT, tiles, 1, (1, 0), (1, 1, hTd4))
    ffn_a(1, hTd4, TL + 2, [(0, 2048, 0)], [0, TL + 1], [])
    tiles = [dict(zi=ti, ty=0, xold=(x3, ti * 128), xnew=(y_out, ti * 128), hcol=None) for ti in range(16)]
    outproj_phase("fout1", FT, hid, tiles, 4, (1, 1), None)
    kb.barrier()
    return nc


def host_consts(core):
    b, s = core // 4, core % 4
    c = {}
    c["c_ident"] = np.eye(128, dtype=np.float32)
    i = np.arange(64)
    mf = (i[:, None] <= i[None, :]).astype(np.float32)
    mb = (i[:, None] >= i[None, :]).astype(np.float32)
    c["c_maskf"] = np.zeros((128, 512), np.float32)
    c["c_maskb"] = np.zeros((128, 512), np.float32)
    for ch in range(8):
        r0 = (ch % 2) * 64
        c["c_maskf"][r0:r0 + 64, ch * 64:(ch + 1) * 64] = mf
        c["c_maskb"][r0:r0 + 64, ch * 64:(ch + 1) * 64] = mb
    rm = np.ones((128, TT), np.float32)
    rm[:, 0::64] = 0.0
    c["c_rmask"] = rm
    e4 = np.zeros((128, 4), np.float32)
    e4[:, s] = 1.0
    c["c_e4"] = e4
    hs = np.zeros((16, 2), np.float32)
    if s > 0:
        hs[2 * (core - 1) + 1, 0] = 1.0
    if s < 3:
        hs[2 * (core + 1), 1] = 1.0
    c["c_hsel"] = hs
    ebv = np.zeros((128, 2), np.float32)
    ebv[:, b] = 1.0
    c["c_eb"] = ebv
    pm = np.zeros((128, 128), np.float32)
    for d in range(128):
        pm[d ^ 1, d] = 1.0
    c["c_perm"] = pm
    t = np.arange(TL) + s * TL
    rows = (t // 64).astype(np.float32)
    cols = (t % 64).astype(np.float32)
    inv = (10000.0 ** (-np.arange(0, 64, 2, dtype=np.float32) / 64.0)).astype(np.float32)
    ang = np.concatenate([rows[:, None] * inv[None], cols[:, None] * inv[None]], axis=-1)
    angd = np.repeat(ang, 2, axis=1).T
    sign = np.where(np.arange(128) % 2 == 0, -1.0, 1.0).astype(np.float32)[:, None]
    c["c_ropec"] = np.cos(angd).astype(np.float32)
    c["c_ropes"] = (np.sin(angd) * sign).astype(np.float32)
    return c


def make_in_maps(inp):
    f = lambda a: np.ascontiguousarray(np.asarray(a, dtype=np.float32))
    x, c, ctx, c_ctx = f(inp["x"]), f(inp["c"]), f(inp["ctx"]), f(inp["c_ctx"])
    def r_hg(w):
        return np.ascontiguousarray(w.reshape(KC, 128, 5, 16, 128).transpose(3, 1, 0, 2, 4).reshape(2048, 10240))

    def r_fin(w):
        return np.ascontiguousarray(w.reshape(KC, 128, 2, FT, 128).transpose(3, 1, 0, 2, 4).reshape(DFF, 4096))

    def r_qkv(w):
        return np.ascontiguousarray(w.reshape(KC, 128, 24, 128).transpose(2, 1, 0, 3).reshape(3072, 2048))

    big = {"adaw0": f(inp["ada_w"][0]), "adaw1": f(inp["ada_w"][1]), "hgwin": r_hg(f(inp["hgrn_w_in"][0])),
           "hgwout": f(inp["hgrn_w_out"][0]), "fin0": r_fin(f(inp["ffn_w_in"][0])), "fout0": f(inp["ffn_w_out"][0]),
           "qkv": r_qkv(f(inp["attn_w_qkv"][0])), "aout": f(inp["attn_w_out"][0]), "fin1": r_fin(f(inp["ffn_w_in"][1])),
           "fout1": f(inp["ffn_w_out"][1])}
    def pad(w, rows):
        return np.concatenate([w, np.zeros((rows - w.shape[0], w.shape[1]), np.float32)], 0) if w.shape[0] < rows else w

    big["fin0"], big["fin1"] = pad(big["fin0"], 6144), pad(big["fin1"], 6144)
    big["fout0"], big["fout1"] = pad(big["fout0"], 6144), pad(big["fout1"], 6144)
    groups = {"g12288": ["adaw0", "adaw1"], "ghg": ["hgwin"], "g2048": ["hgwout", "aout", "qkv", "fout0", "fout1"],
              "g4096": ["fin0", "fin1"]}
    maps = []
    for core in range(NCORE):
        b, s = core // 4, core % 4
        m = {}
        m["x_in"] = np.ascontiguousarray(x[b, s * TL:(s + 1) * TL])
        m["ctx_in"] = np.ascontiguousarray(ctx[b])
        cv = np.stack([c[b], c_ctx], axis=-1)
        m["cvec"] = np.ascontiguousarray(cv.reshape(KC, 128, 2).transpose(1, 0, 2))
        for gname, nms in groups.items():
            parts = []
            for nm in nms:
                w = big[nm]
                r = w.shape[0] // 8
                parts.append(w[core * r:(core + 1) * r])
            m[gname + "_s"] = np.ascontiguousarray(np.concatenate(parts, 0))
        m["adab"] = f(inp["ada_b"])
        m["n_mpre"], m["n_mpost"] = f(inp["norm_mix_pre"]), f(inp["norm_mix_post"])
        m["n_fpre"], m["n_fpost"] = f(inp["norm_ffn_pre"]), f(inp["norm_ffn_post"])
        m["lbl"] = f(inp["hgrn_lb_logits"])
        m["ogain"] = f(inp["hgrn_o_norm"][0]).reshape(128, 1)
        m["qgain"] = f(inp["attn_q_norm"][0]).reshape(128, 1)
        m["kgain"] = f(inp["attn_k_norm"][0]).reshape(128, 1)
        m["convw"], m["convb"] = f(inp["ffn_conv_w"]), f(inp["ffn_conv_b"])
        m.update(host_consts(core))
        maps.append(m)
    return maps


_NC = None


def kernel(**inputs):
    global _NC
    if _NC is None:
        _NC = build()
    maps = make_in_maps(inputs)
    res = run_bass_kernel_spmd(_NC, maps, core_ids=list(range(NCORE)))
    out = np.zeros((2, 4 * TL, D), np.float32)
    for core in range(NCORE):
        b, s = core // 4, core % 4
        out[b, s * TL:(s + 1) * TL] = res.results[core]["y_out"]
    return out
```
